# Optimizing a Trainium2 kernel written in Bass

```python
import jax, jax.numpy as jnp
from jax import lax
import numpy as np

D_MODEL = 4096
BATCH = 32
SEQ = 256
DEPTH = 1
DEC_BATCH = 4
DEC_SEQ = 2048
PAST_LEN = 512

GRID_W = 64
MIX_WIDTH = D_MODEL
RET_DK = 256
RET_DV = 256
RET_HEADS = D_MODEL // 512
RET_WIDTH = RET_HEADS * RET_DV
GDN_DK = 128
GDN_DV = 128
GDN_HEADS = D_MODEL // 256
GDN_WIDTH = GDN_HEADS * GDN_DV
CONV_K = 5
CHUNK = 64
ROPE_BASE = 10000.0
EPS = 1e-6
PROJ_SIZES = (RET_HEADS * RET_DK, RET_HEADS * RET_DK, RET_WIDTH, RET_WIDTH,
              GDN_HEADS * GDN_DK * 2 + GDN_WIDTH, GDN_WIDTH,
              GDN_HEADS, GDN_HEADS, GDN_HEADS, GDN_HEADS)
PROJ_DIM = sum(PROJ_SIZES)

kernel_name = "hybrid_retention_gdn_diffusion_step"


def _rms_norm(x, w):
    xf = x.astype(jnp.float32)
    y = xf * lax.rsqrt(jnp.mean(xf * xf, axis=-1, keepdims=True) + EPS)
    return (y * w.astype(jnp.float32)).astype(x.dtype)


def _head_rms(x):
    return x * lax.rsqrt(jnp.mean(x * x, axis=-1, keepdims=True) + EPS)


def _l2norm(x):
    return x * lax.rsqrt(jnp.sum(x * x, axis=-1, keepdims=True) + EPS)


def _split_proj(proj):
    parts, start = [], 0
    for size in PROJ_SIZES:
        parts.append(proj[..., start:start + size])
        start += size
    return parts


def _to_chunks(x):
    b, t, h, d = x.shape
    return x.reshape(b, t // CHUNK, CHUNK, h, d).transpose(1, 0, 3, 2, 4)


def _to_chunks_heads(x):
    b, t, h = x.shape
    return x.reshape(b, t // CHUNK, CHUNK, h).transpose(1, 0, 3, 2)


def _from_chunks(x):
    n, b, h, c, d = x.shape
    return x.transpose(1, 0, 3, 2, 4).reshape(b, n * c, h, d)


def _axial_rope(x):
    b, t, h, d = x.shape
    rows = t // GRID_W
    row = jnp.repeat(jnp.arange(rows), GRID_W).astype(jnp.float32)
    col = jnp.tile(jnp.arange(GRID_W), rows).astype(jnp.float32)
    quarter = d // 4
    half = d // 2
    inv_freq = ROPE_BASE ** (-jnp.arange(quarter, dtype=jnp.float32) / quarter)

    def rot(xh, pos):
        ang = pos[:, None] * inv_freq[None, :]
        cos = jnp.cos(ang)[None, :, None, :]
        sin = jnp.sin(ang)[None, :, None, :]
        x1, x2 = xh[..., :quarter], xh[..., quarter:]
        return jnp.concatenate([x1 * cos - x2 * sin, x1 * sin + x2 * cos], axis=-1)

    return jnp.concatenate([rot(x[..., :half], row), rot(x[..., half:], col)], axis=-1)


def _centred_dwconv(x, w):
    pad = CONV_K // 2
    t = x.shape[1]
    xp = jnp.pad(x, ((0, 0), (pad, pad), (0, 0)))
    acc = xp[:, 0:t] * w[0]
    for i in range(1, CONV_K):
        acc = acc + xp[:, i:i + t] * w[i]
    return acc


def _retention_chunked(q, k, v, log_gamma, s0):
    idx = jnp.arange(CHUNK, dtype=jnp.float32)
    diff = idx[:, None] - idx[None, :]
    causal = diff >= 0
    intra = jnp.where(causal, jnp.exp(jnp.where(causal, diff, 0.0)[None] * log_gamma[:, None, None]), 0.0)
    q_dec = jnp.exp((idx + 1.0)[None, :] * log_gamma[:, None])[..., None]
    k_dec = jnp.exp((CHUNK - 1.0 - idx)[None, :] * log_gamma[:, None])[..., None]
    c_dec = jnp.exp(CHUNK * log_gamma)[:, None, None]

    def step(s, xs):
        qi, ki, vi = xs
        scores = jnp.einsum('bhcd,bhsd->bhcs', qi, ki) * intra
        o = (jnp.einsum('bhcs,bhsv->bhcv', scores, vi)
             + jnp.einsum('bhcd,bhdv->bhcv', qi * q_dec, s))
        s = s * c_dec + jnp.einsum('bhcd,bhcv->bhdv', ki * k_dec, vi)
        return s, o

    s_final, o = lax.scan(step, s0, (_to_chunks(q), _to_chunks(k), _to_chunks(v)))
    return _from_chunks(o), s_final


def _gated_delta_chunked(q, k, v, g, beta, s0):
    qc, kc, vc = _to_chunks(q), _to_chunks(k), _to_chunks(v)
    gc = jnp.cumsum(_to_chunks_heads(g), axis=-1)
    bc = _to_chunks_heads(beta)[..., None]
    idx = jnp.arange(CHUNK)
    incl = idx[:, None] >= idx[None, :]
    strict = idx[:, None] > idx[None, :]
    decay = jnp.exp(jnp.where(incl, gc[..., :, None] - gc[..., None, :], -jnp.inf))
    kb, vb = kc * bc, vc * bc
    lower = jnp.where(strict, jnp.einsum('nbhcd,nbhsd->nbhcs', kb, kc) * decay, 0.0)
    eye = jnp.eye(CHUNK, dtype=jnp.float32)
    tmat = lax.linalg.triangular_solve(eye + lower, jnp.broadcast_to(eye, lower.shape),
                                       left_side=True, lower=True)
    u = jnp.einsum('nbhcs,nbhsv->nbhcv', tmat, vb)
    w = jnp.einsum('nbhcs,nbhsd->nbhcd', tmat, kb * jnp.exp(gc)[..., None])
    qk = jnp.einsum('nbhcd,nbhsd->nbhcs', qc, kc) * decay
    g_last = gc[..., -1]
    q_head = qc * jnp.exp(gc)[..., None]
    k_tail = kc * jnp.exp(g_last[..., None] - gc)[..., None]

    def step(s, xs):
        qk_i, u_i, w_i, qh_i, kt_i, gl_i = xs
        v_new = u_i - jnp.einsum('bhcd,bhdv->bhcv', w_i, s)
        o = jnp.einsum('bhcd,bhdv->bhcv', qh_i, s) + jnp.einsum('bhcs,bhsv->bhcv', qk_i, v_new)
        s = s * jnp.exp(gl_i)[..., None, None] + jnp.einsum('bhcd,bhcv->bhdv', kt_i, v_new)
        return s, o

    s_final, o = lax.scan(step, s0, (qk, u, w, q_head, k_tail, g_last))
    return _from_chunks(o), s_final


def _mixer(u, s_rf, s_rb, s_gf, s_gb, w_in, ret_dec_f, ret_dec_b, conv_w,
           a_log_f, a_log_b, dtb_f, dtb_b, gdn_norm_w, w_out, grid_positions):
    f32 = jnp.float32
    b, t, _ = u.shape
    proj = u @ w_in
    rq, rk, rv, rg, gqkv, gz, af, ab, bfw, bbw = _split_proj(proj)
    flip = lambda a: jnp.flip(a, axis=1)

    rq = rq.reshape(b, t, RET_HEADS, RET_DK).astype(f32)
    rk = rk.reshape(b, t, RET_HEADS, RET_DK).astype(f32)
    rv = rv.reshape(b, t, RET_HEADS, RET_DV).astype(f32)
    if grid_positions:
        rq, rk = _axial_rope(rq), _axial_rope(rk)
    rq = rq * (RET_DK ** -0.5)
    lg_f = jax.nn.log_sigmoid(ret_dec_f.astype(f32))
    lg_b = jax.nn.log_sigmoid(ret_dec_b.astype(f32))
    o_rf, s_rf_new = _retention_chunked(rq, rk, rv, lg_f, s_rf.astype(f32))
    o_rb, s_rb_new = _retention_chunked(flip(rq), flip(rk), flip(rv), lg_b, s_rb.astype(f32))
    o_ret = _head_rms(o_rf + flip(o_rb)).reshape(b, t, RET_WIDTH) * jax.nn.silu(rg.astype(f32))

    gqkv = jax.nn.silu(_centred_dwconv(gqkv, conv_w)).astype(f32)
    nqk = GDN_HEADS * GDN_DK
    gq = _l2norm(gqkv[..., :nqk].reshape(b, t, GDN_HEADS, GDN_DK)) * (GDN_DK ** -0.5)
    gk = _l2norm(gqkv[..., nqk:2 * nqk].reshape(b, t, GDN_HEADS, GDN_DK))
    gv = gqkv[..., 2 * nqk:].reshape(b, t, GDN_HEADS, GDN_DV)
    g_f = -jnp.exp(a_log_f.astype(f32)) * jax.nn.softplus(af.astype(f32) + dtb_f.astype(f32))
    g_b = -jnp.exp(a_log_b.astype(f32)) * jax.nn.softplus(ab.astype(f32) + dtb_b.astype(f32))
    beta_f = jax.nn.sigmoid(bfw.astype(f32))
    beta_b = jax.nn.sigmoid(bbw.astype(f32))
    o_gf, s_gf_new = _gated_delta_chunked(gq, gk, gv, g_f, beta_f, s_gf.astype(f32))
    o_gb, s_gb_new = _gated_delta_chunked(flip(gq), flip(gk), flip(gv), flip(g_b), flip(beta_b),
                                          s_gb.astype(f32))
    o_gdn = _rms_norm(o_gf + flip(o_gb), gdn_norm_w) * jax.nn.silu(
        gz.reshape(b, t, GDN_HEADS, GDN_DV).astype(f32))
    o_gdn = o_gdn.reshape(b, t, GDN_WIDTH)

    mixed = jnp.concatenate([o_ret, o_gdn], axis=-1).astype(u.dtype)
    return mixed @ w_out, (s_rf_new, s_rb_new, s_gf_new, s_gb_new)


def setup_inputs(seed: int = 0) -> dict:
    key = jax.random.key(seed)
    ks = jax.random.split(key, 24)
    f32 = jnp.float32
    nrm = lambda k, shape, s: jax.random.normal(k, shape, f32) * s
    dec_init = jnp.asarray(np.log(2.0 ** (5.0 + np.arange(RET_HEADS)) - 1.0), f32)
    dt_lo, dt_hi = np.log(1e-3), np.log(1e-1)
    dt_f = jnp.exp(jax.random.uniform(ks[15], (DEPTH, GDN_HEADS), f32) * (dt_hi - dt_lo) + dt_lo)
    dt_b = jnp.exp(jax.random.uniform(ks[16], (DEPTH, GDN_HEADS), f32) * (dt_hi - dt_lo) + dt_lo)
    inv_softplus = lambda d: d + jnp.log(-jnp.expm1(-d))
    return {
        "x_prompt": nrm(ks[0], (BATCH, SEQ, D_MODEL), 1.0),
        "x_sample": nrm(ks[1], (DEC_BATCH, DEC_SEQ, D_MODEL), 1.0),
        "state_ret_fwd": nrm(ks[2], (DEC_BATCH, DEPTH, RET_HEADS, RET_DK, RET_DV), 0.1),
        "state_ret_bwd": nrm(ks[3], (DEC_BATCH, DEPTH, RET_HEADS, RET_DK, RET_DV), 0.1),
        "state_gdn_fwd": nrm(ks[4], (DEC_BATCH, DEPTH, GDN_HEADS, GDN_DK, GDN_DV), 0.1),
        "state_gdn_bwd": nrm(ks[5], (DEC_BATCH, DEPTH, GDN_HEADS, GDN_DK, GDN_DV), 0.1),
        "c": nrm(ks[6], (DEC_BATCH, D_MODEL), 1.0),
        "c_ctx": nrm(ks[7], (D_MODEL,), 1.0),
        "w_ada": nrm(ks[8], (DEPTH, D_MODEL, 3 * D_MODEL), D_MODEL ** -0.5),
        "b_ada": nrm(ks[9], (DEPTH, 3 * D_MODEL), 0.01),
        "norm_w": 1.0 + nrm(ks[10], (DEPTH, D_MODEL), 0.02),
        "w_in": nrm(ks[11], (DEPTH, D_MODEL, PROJ_DIM), D_MODEL ** -0.5),
        "ret_decay_fwd": dec_init[None, :] + nrm(ks[12], (DEPTH, RET_HEADS), 0.05),
        "ret_decay_bwd": dec_init[None, :] + nrm(ks[13], (DEPTH, RET_HEADS), 0.05),
        "conv_w": nrm(ks[14], (DEPTH, CONV_K, 2 * GDN_HEADS * GDN_DK + GDN_WIDTH), CONV_K ** -0.5),
        "gdn_a_log_fwd": jnp.log(jax.random.uniform(ks[17], (DEPTH, GDN_HEADS), f32, 1.0, 16.0)),
        "gdn_a_log_bwd": jnp.log(jax.random.uniform(ks[18], (DEPTH, GDN_HEADS), f32, 1.0, 16.0)),
        "gdn_dt_bias_fwd": inv_softplus(dt_f),
        "gdn_dt_bias_bwd": inv_softplus(dt_b),
        "gdn_norm_w": 1.0 + nrm(ks[19], (DEPTH, GDN_DV), 0.02),
        "w_out": nrm(ks[20], (DEPTH, MIX_WIDTH, D_MODEL), MIX_WIDTH ** -0.5),
        "final_norm_w": 1.0 + nrm(ks[21], (D_MODEL,), 0.02),
    }


def reference(x_prompt, x_sample, state_ret_fwd, state_ret_bwd, state_gdn_fwd, state_gdn_bwd, c,
              c_ctx, w_ada, b_ada, norm_w, w_in, ret_decay_fwd, ret_decay_bwd, conv_w,
              gdn_a_log_fwd, gdn_a_log_bwd, gdn_dt_bias_fwd, gdn_dt_bias_bwd, gdn_norm_w,
              w_out, final_norm_w):
    f32 = jnp.float32

    def layer_params(l):
        return (w_in[l], ret_decay_fwd[l], ret_decay_bwd[l], conv_w[l], gdn_a_log_fwd[l],
                gdn_a_log_bwd[l], gdn_dt_bias_fwd[l], gdn_dt_bias_bwd[l], gdn_norm_w[l], w_out[l])

    h = x_prompt
    bp = x_prompt.shape[0]
    z_ret = jnp.zeros((bp, RET_HEADS, RET_DK, RET_DV), f32)
    z_gdn = jnp.zeros((bp, GDN_HEADS, GDN_DK, GDN_DV), f32)
    rf_list, rb_list, gf_list, gb_list = [], [], [], []
    for l in range(DEPTH):
        mod = jax.nn.silu(c_ctx) @ w_ada[l] + b_ada[l]
        shift, scale, gate = jnp.split(mod, 3, axis=-1)
        u = _rms_norm(h, norm_w[l]) * (1.0 + scale) + shift
        out, (srf, srb, sgf, sgb) = _mixer(u, z_ret, z_ret, z_gdn, z_gdn, *layer_params(l),
                                           grid_positions=False)
        h = h + gate * out
        rf_list.append(srf); rb_list.append(srb); gf_list.append(sgf); gb_list.append(sgb)
    y_prompt = _rms_norm(h, final_norm_w)
    new_ret_fwd = jnp.stack(rf_list, axis=1)
    new_ret_bwd = jnp.stack(rb_list, axis=1)
    new_gdn_fwd = jnp.stack(gf_list, axis=1)
    new_gdn_bwd = jnp.stack(gb_list, axis=1)

    h = x_sample
    for l in range(DEPTH):
        mod = (jax.nn.silu(c) @ w_ada[l] + b_ada[l])[:, None, :]
        shift, scale, gate = jnp.split(mod, 3, axis=-1)
        u = _rms_norm(h, norm_w[l]) * (1.0 + scale) + shift
        out, _ = _mixer(u, state_ret_fwd[:, l], state_ret_bwd[:, l], state_gdn_fwd[:, l],
                        state_gdn_bwd[:, l], *layer_params(l), grid_positions=True)
        h = h + gate * out
    y_sample = _rms_norm(h, final_norm_w)

    return (y_prompt, y_sample, new_ret_fwd, new_ret_bwd, new_gdn_fwd, new_gdn_bwd)
```

```python
import math
from contextlib import ExitStack

import numpy as np
import concourse.bass as bass
import concourse.mybir as mybir
from concourse.bass_utils import run_bass_kernel_spmd

F32 = mybir.dt.float32
BF16 = mybir.dt.bfloat16
AF = mybir.ActivationFunctionType
ALU = mybir.AluOpType
AX = mybir.AxisListType

D = 4096
NTOK = 2048
PROJ = 16448
EPS = 1e-6
ENGS = ("pe", "act", "dve", "pool", "sp")
DMA_K = 12


class Tok:
    __slots__ = ("eng", "idx", "is_dma", "sem", "val", "needed", "uid")

    def __init__(self, eng, idx, is_dma=False, sem=None, val=0, uid=0):
        self.eng, self.idx, self.is_dma, self.sem, self.val, self.uid = eng, idx, is_dma, sem, val, uid
        self.needed = False


class Op:
    __slots__ = ("fn", "waits", "tok", "prewait")

    def __init__(self, fn, waits, tok, prewait=None):
        self.fn, self.waits, self.tok, self.prewait = fn, waits, tok, prewait


class Reg:
    __slots__ = ("w", "r")

    def __init__(self):
        self.w = None
        self.r = []


class Sched:
    def __init__(self):
        self.ops = {e: [] for e in ENGS}
        self.reg = {}
        self.known = {e: {f: -1 for f in ENGS} for e in ENGS}
        self.known_dma = {e: set() for e in ENGS}
        self.dma_count = {e: 0 for e in ENGS}
        self.uid = 0
        self.out_dma = []
        self.live_dma = []

    def add(self, eng, fn, reads=(), writes=(), dma=False, is_out=False, extra=()):
        idx = len(self.ops[eng])
        deps = {}
        for r in reads:
            st = self.reg.get(r)
            if st is not None and st.w is not None:
                deps[id(st.w)] = (st.w, True)
        for w in writes:
            st = self.reg.get(w)
            if st is not None:
                if st.w is not None:
                    deps[id(st.w)] = (st.w, True)
                for t in st.r:
                    if id(t) not in deps:
                        deps[id(t)] = (t, False)
        for t in extra:
            deps[id(t)] = (t, True)
        waits = []
        for d, is_raw in deps.values():
            if d.is_dma:
                if d.uid in self.known_dma[eng]:
                    continue
                self.known_dma[eng].add(d.uid)
                waits.append(d)
            else:
                if d.eng == eng:
                    if eng in ("pe", "sp") or not is_raw:
                        continue
                if d.idx <= self.known[eng][d.eng]:
                    continue
                self.known[eng][d.eng] = d.idx
                d.needed = True
                waits.append(d)
        self.uid += 1
        prewait = None
        if dma:
            i = self.dma_count[eng]
            self.dma_count[eng] = i + 1
            tok = Tok(eng, idx, True, (eng, i % DMA_K), 16 * (i // DMA_K + 1), self.uid)
            if i >= DMA_K:
                prewait = ((eng, i % DMA_K), 16 * (i // DMA_K))
            if is_out:
                self.out_dma.append(tok)
            self.live_dma.append(tok)
        else:
            tok = Tok(eng, idx, False, None, 0, self.uid)
        self.ops[eng].append(Op(fn, waits, tok, prewait))
        for r in reads:
            st = self.reg.get(r)
            if st is None:
                st = self.reg[r] = Reg()
            st.r.append(tok)
        for w in writes:
            st = self.reg.get(w)
            if st is None:
                st = self.reg[w] = Reg()
            st.w = tok
            st.r = []
        return tok

    def pe(self, fn, reads=(), writes=()):
        return self.add("pe", fn, reads, writes)

    def act(self, fn, reads=(), writes=()):
        return self.add("act", fn, reads, writes)

    def dve(self, fn, reads=(), writes=()):
        return self.add("dve", fn, reads, writes)

    def pool(self, fn, reads=(), writes=()):
        return self.add("pool", fn, reads, writes)

    def dma(self, q, out, in_, reads=(), writes=(), is_out=False, **kw):
        return self.add(q, lambda e: e.dma_start(out=out, in_=in_, **kw), reads, writes,
                        dma=True, is_out=is_out)

    def barrier(self, scratch_ap):
        extra = []
        for e in ENGS:
            if self.ops[e]:
                for op in reversed(self.ops[e]):
                    if not op.tok.is_dma and op.fn is not None:
                        extra.append(op.tok)
                        break
        extra += self.live_dma
        self.live_dma = []
        tok = self.add("dve", lambda e: e.memset(scratch_ap, 0.0), extra=extra)
        for e in ("pe", "act", "pool", "sp"):
            self.add(e, None, extra=[tok])
        self.reg = {}

    def finish(self):
        self.ops["sp"].append(Op(None, list(self.out_dma), Tok("sp", len(self.ops["sp"])), None))

    def emit(self, nc, stack):
        esem = {e: stack.enter_context(nc.semaphore("es_" + e)) for e in ENGS}
        dsem = {}
        for q in ENGS:
            for k in range(min(DMA_K, self.dma_count[q])):
                dsem[(q, k)] = stack.enter_context(nc.semaphore("ds_%s_%d" % (q, k)))
        cnt = {}
        for e in ENGS:
            c = 0
            arr = []
            for op in self.ops[e]:
                if op.tok.needed and not op.tok.is_dma:
                    c += 1
                arr.append(c)
            cnt[e] = arr
        block = stack.enter_context(nc.Block())

        def replay(e_name, e):
            for op in self.ops[e_name]:
                if op.prewait is not None:
                    e.wait_ge(dsem[op.prewait[0]], op.prewait[1])
                for d in op.waits:
                    if d.is_dma:
                        e.wait_ge(dsem[d.sem], d.val)
                    else:
                        e.wait_ge(esem[d.eng], cnt[d.eng][d.idx])
                if op.fn is None:
                    continue
                ins = op.fn(e)
                if op.tok.is_dma:
                    ins.then_inc(dsem[op.tok.sem], 16)
                elif op.tok.needed:
                    ins.then_inc(esem[e_name], 1)

        @block.tensor
        def _(e):
            replay("pe", e)

        @block.scalar
        def _(e):
            replay("act", e)

        @block.vector
        def _(e):
            replay("dve", e)

        @block.gpsimd
        def _(e):
            replay("pool", e)

        @block.sync
        def _(e):
            replay("sp", e)


class Arena:
    def __init__(self, t, nwords):
        self.t, self.n, self.top = t, nwords, 0

    def alloc(self, shape, dtype=F32):
        n = 1
        for s in shape[1:]:
            n *= s
        words = n if dtype == F32 else (n + 1) // 2
        off = self.top
        self.top += words
        assert self.top <= self.n, "SBUF arena overflow %d > %d" % (self.top, self.n)
        ap = self.t[:, off:off + words]
        if dtype != F32:
            ap = ap.bitcast(dtype)
        if len(shape) == 3:
            ap = ap.rearrange("p (a b) -> p a b", a=shape[1])
        elif len(shape) == 4:
            ap = ap.rearrange("p (a b c) -> p a b c", a=shape[1], b=shape[2])
        if shape[0] != 128:
            ap = ap[0:shape[0]]
        return ap


def build_nc(debug=None):
    nc = bass.Bass("TRN2", target_bir_lowering=False)

    def din(name, shape, dt=F32):
        return nc.dram_tensor(name, list(shape), dt, kind="ExternalInput").ap()

    def dout(name, shape, dt=F32):
        return nc.dram_tensor(name, list(shape), dt, kind="ExternalOutput").ap()

    def dscr(name, shape, dt=F32):
        kind = "ExternalOutput" if (debug and name in debug) else "Internal"
        if debug and "only3b" in debug and name in ("gqT", "gkT", "gk_tok", "gv_tok", "gz_tok"):
            kind = "ExternalInput"
        return nc.dram_tensor(name, list(shape), dt, kind=kind).ap()

    x = din("x", [NTOK, D])
    cv = din("cv", [128, 32])
    w_ada = din("w_ada", [D, 3 * D])
    b_ada = din("b_ada", [1, 3 * D])
    nw = din("nw", [128, 32])
    w_in = din("w_in", [D, PROJ])
    w_out = din("w_out", [D, D])
    fnw = din("fnw", [1, D])
    consts = din("consts", [128, 21 * 128])
    misc = din("misc", [128, 8])
    rope = din("rope", [128, 192])
    retdec = din("retdec", [1, 16])
    convw = din("convw", [128, 48, 5])
    gpar = din("gpar", [32, 2])
    gnw = din("gnw", [1, 128])
    s_rf = din("s_rf", [8, 256, 256])
    s_rb = din("s_rb", [8, 256, 256])
    s_gf = din("s_gf", [16, 128, 128])
    s_gb = din("s_gb", [16, 128, 128])

    y = dout("y", [NTOK, D])
    o_rf = dout("o_rf", [8, 8, 256, 256])
    o_rb = dout("o_rb", [8, 8, 256, 256])
    o_gf = dout("o_gf", [8, 16, 128, 128])
    o_gb = dout("o_gb", [8, 16, 128, 128])

    projT = dscr("projT", [PROJ, NTOK])
    gate_d = dscr("gate_d", [1, D])
    mixed = dscr("mixed", [NTOK, D])
    gqT = dscr("gqT", [16, 128, NTOK], BF16)
    gkT = dscr("gkT", [16, 128, NTOK], BF16)
    gk_tok = dscr("gk_tok", [NTOK, 2048])
    gv_tok = dscr("gv_tok", [NTOK, 2048])
    gz_tok = dscr("gz_tok", [NTOK, 2048])
    ob_scr = dscr("ob_scr", [NTOK, 2048])
    delta = dscr("delta", [NTOK, D])

    S = Sched()
    with ExitStack() as st:
        NW = 53200
        arena_t = st.enter_context(nc.sbuf_tensor("arena", [128, NW], F32))
        AR = Arena(arena_t, NW)
        pst = st.enter_context(nc.psum_tensor("pst", [128, 4096], F32))

        def PS(b, lo=0, hi=512, p0=0, p1=128):
            return pst[p0:p1, b * 512 + lo:b * 512 + hi]

        def PB(b):
            return "psb%d" % b

        cst = AR.alloc([128, 21 * 128])
        S.dma("sp", cst, consts, writes=["cst"])

        def CT(i):
            return cst[:, i * 128:(i + 1) * 128]
        ident, perm = CT(0), CT(1)
        triF, triB = CT(2), CT(3)
        allowF, allowB = CT(4), CT(5)
        strictF, strictB = CT(6), CT(7)
        negmF, negmB = CT(8), CT(9)
        DF, DB = CT(10), CT(11)
        qeF, qeB = CT(12), CT(13)
        ones = AR.alloc([128, 128])
        S.dve(lambda e: e.memset(ones, 1.0), writes=["ones"])
        msc = AR.alloc([128, 8])
        S.dma("sp", msc, misc, writes=["msc"])
        carry = msc[:, 0:1]
        small = AR.alloc([128, 64])
        Acol = AR.alloc([128, 32])
        Bcol = AR.alloc([128, 32])
        gb_tok = AR.alloc([128, 16, 64])
        bar = AR.alloc([128, 2])
        persist_top = AR.top
        bankctr = [0]

        def nb():
            bankctr[0] = (bankctr[0] + 1) % 8
            return bankctr[0]

        class _Dummy:
            def __getattr__(self, name):
                return lambda *a, **k: None
        S_real = S
        if debug is not None and "only3b" in debug:
            S = _Dummy()

        cvt = AR.alloc([128, 32])
        sig = AR.alloc([128, 32])
        sc = AR.alloc([128, 32])
        nwt = AR.alloc([128, 32])
        brows = [AR.alloc([128, 4096]) for _ in range(3)]
        modrow = AR.alloc([128, 4096])
        wb = [AR.alloc([128, 4096]) for _ in range(3)]
        S.dma("sp", cvt, cv, writes=["cvt"])
        S.dma("sp", nwt, nw, writes=["nwt"])
        S.act(lambda e: e.activation(out=sig, in_=cvt, func=AF.Sigmoid), reads=["cvt"], writes=["sig"])
        S.dve(lambda e: e.tensor_tensor(out=sc, in0=cvt, in1=sig, op=ALU.mult), reads=["cvt", "sig"], writes=["sc"])
        allb = [PB(b) for b in range(8)]
        for r in range(3):
            brow = brows[r]
            S.dma("sp", brow[0:1, :], b_ada[0:1, r * D:(r + 1) * D], writes=["brow%d" % r])
            for kc in range(32):
                wt = wb[kc % 3]
                wn = "wb%d" % (kc % 3)
                S.dma("sp", wt, w_ada[kc * 128:(kc + 1) * 128, r * D:(r + 1) * D], writes=[wn])
                for n in range(8):
                    S.pe(lambda e, wt=wt, n=n, kc=kc: e.matmul(
                        PS(n, p1=1), lhsT=sc[:, kc:kc + 1], rhs=wt[:, n * 512:(n + 1) * 512],
                        start=(kc == 0), stop=(kc == 31)), reads=[wn, "sc"], writes=[PB(n)])
            S.dve(lambda e, brow=brow: e.tensor_tensor(out=modrow[0:1, :], in0=pst[0:1, :], in1=brow[0:1, :], op=ALU.add),
                  reads=["brow%d" % r], writes=["modrow"] + allb)
            if r < 2:
                for kc in range(32):
                    S.pe(lambda e, kc=kc: e.matmul(PS(0, kc, kc + 1), lhsT=modrow[0:1, kc * 128:(kc + 1) * 128],
                                                   rhs=ones[0:1, 0:1], start=True, stop=True),
                         reads=["modrow", "ones"], writes=[PB(0)])
                if r == 0:
                    S.act(lambda e: e.copy(out=Bcol, in_=PS(0, 0, 32)), writes=["Bcol", PB(0)])
                else:
                    S.dve(lambda e: e.scalar_tensor_tensor(out=Acol, in0=PS(0, 0, 32), scalar=1.0, in1=nwt,
                                                           op0=ALU.add, op1=ALU.mult),
                          reads=["nwt"], writes=["Acol", PB(0)])
            else:
                S.dma("sp", gate_d, modrow[0:1, :], reads=["modrow"], writes=["gate_d"])

        S.barrier(bar[:, 0:1])
        AR.top = persist_top
        uT = AR.alloc([128, 32, NTOK], BF16)
        ph1_top = AR.top
        xb = [AR.alloc([128, 4096]) for _ in range(2)]
        xn2 = [AR.alloc([128, 4096]) for _ in range(2)]
        ssc = AR.alloc([128, 48])
        def x0b(tt):
            xt = xb[tt % 2]
            xnm = "xb%d" % (tt % 2)
            xn = xn2[tt % 2]
            xnn = "xn%d" % (tt % 2)
            S.dma("sp", xt, x[tt * 128:(tt + 1) * 128, :], writes=[xnm])
            S.act(lambda e, xt=xt, tt=tt, xn=xn: e.activation(out=xn, in_=xt, func=AF.Square, accum_out=ssc[:, tt:tt + 1]),
                  reads=[xnm], writes=[xnn, "ssc"])
            S.act(lambda e, tt=tt: e.activation(out=ssc[:, 16 + tt:17 + tt], in_=ssc[:, tt:tt + 1], func=AF.Sqrt,
                                                scale=1.0 / D, bias=EPS), reads=["ssc"], writes=["ssc"])
            S.dve(lambda e, tt=tt: e.reciprocal(out=ssc[:, 32 + tt:33 + tt], in_=ssc[:, 16 + tt:17 + tt]),
                  reads=["ssc"], writes=["ssc"])
            S.dve(lambda e, xt=xt, tt=tt, xn=xn: e.tensor_scalar(out=xn, in0=xt, scalar1=ssc[:, 32 + tt:33 + tt], scalar2=None,
                                                          op0=ALU.mult), reads=[xnm, "ssc"], writes=[xnn])
        def y0b(tt):
            xn = xn2[tt % 2]
            xnn = "xn%d" % (tt % 2)
            for g in range(8):
                for j in range(4):
                    kc = g * 4 + j
                    S.pe(lambda e, g=g, j=j, kc=kc, xn=xn: e.transpose(out=PS(g, j * 128, (j + 1) * 128),
                                                                 in_=xn[:, kc * 128:(kc + 1) * 128], identity=ident),
                         reads=[xnn, "cst"], writes=[PB(g)])
                for j in range(4):
                    kc = g * 4 + j
                    dst = uT[:, kc, tt * 128:(tt + 1) * 128]
                    if g % 2 == 0:
                        S.act(lambda e, g=g, j=j, kc=kc, dst=dst: e.activation(
                            out=dst, in_=PS(g, j * 128, (j + 1) * 128), func=AF.Identity,
                            scale=Acol[:, kc:kc + 1], bias=Bcol[:, kc:kc + 1]),
                            reads=["Acol", "Bcol"], writes=["uT", PB(g)])
                    else:
                        S.dve(lambda e, g=g, j=j, kc=kc, dst=dst: e.tensor_scalar(
                            out=dst, in0=PS(g, j * 128, (j + 1) * 128), scalar1=Acol[:, kc:kc + 1],
                            scalar2=Bcol[:, kc:kc + 1], op0=ALU.mult, op1=ALU.add),
                            reads=["Acol", "Bcol"], writes=["uT", PB(g)])

        x0b(0)
        for tt in range(16):
            if tt + 1 < 16:
                x0b(tt + 1)
            y0b(tt)

        S.barrier(bar[:, 0:1])
        AR.top = ph1_top
        w32 = [AR.alloc([128, 32, 128]) for _ in range(2)]
        wbf = [AR.alloc([128, 32, 128], BF16) for _ in range(2)]
        ev = AR.alloc([128, NTOK])
        NCG = 129
        for cg in range(NCG):
            s = cg % 2
            ncol = 128 if cg < 128 else 64
            src = w_in[:, cg * 128:cg * 128 + ncol].rearrange("(kc p) c -> p kc c", p=128)
            S.dma("sp", w32[s][:, :, 0:ncol], src, writes=["w32_%d" % s])
            S.dve(lambda e, s=s, ncol=ncol: e.tensor_copy(out=wbf[s][:, :, 0:ncol], in_=w32[s][:, :, 0:ncol]),
                  reads=["w32_%d" % s], writes=["wbf_%d" % s])
            for kc in range(32):
                for tb in range(4):
                    b = s * 4 + tb
                    S.pe(lambda e, s=s, kc=kc, tb=tb, b=b, ncol=ncol: e.matmul(
                        PS(b, p1=ncol), lhsT=wbf[s][:, kc, 0:ncol], rhs=uT[:, kc, tb * 512:(tb + 1) * 512],
                        start=(kc == 0), stop=(kc == 31)), reads=["wbf_%d" % s, "uT"], writes=[PB(b)])
            for tb in range(4):
                b = s * 4 + tb
                S.act(lambda e, b=b, tb=tb, ncol=ncol: e.copy(out=ev[0:ncol, tb * 512:(tb + 1) * 512], in_=PS(b, p1=ncol)),
                      writes=["ev", PB(b)])
            S.dma("pool", projT[cg * 128:cg * 128 + ncol, :], ev[0:ncol, :], reads=["ev"], writes=["projT"])
        S.barrier(bar[:, 0:1])

        if debug is not None and "stop1" in debug:

            S.finish()
            S.emit(nc, st)
            return nc


        AR.top = persist_top
        ropeT = AR.alloc([128, 192])
        S.dma("sp", ropeT, rope, writes=["ropeT"])
        rd = AR.alloc([128, 16])
        lgbc = AR.alloc([128, 16])
        kdec = AR.alloc([128, 16])
        cdec = AR.alloc([128, 16])
        qdt = AR.alloc([128, 16, 128])
        maskT = AR.alloc([128, 8, 128])
        mt1 = AR.alloc([128, 128])
        mt2 = AR.alloc([128, 128])
        S.dma("sp", rd, retdec[0:1, :].to_broadcast([128, 16]), writes=["rd"])
        S.act(lambda e: e.activation(out=rd, in_=rd, func=AF.Exp, scale=-1.0), reads=["rd"], writes=["rd"])
        S.act(lambda e: e.activation(out=rd, in_=rd, func=AF.Ln, bias=1.0), reads=["rd"], writes=["rd"])
        S.dve(lambda e: e.tensor_scalar(out=lgbc, in0=rd, scalar1=-1.0, scalar2=None, op0=ALU.mult),
              reads=["rd"], writes=["lgbc"])
        S.act(lambda e: e.activation(out=kdec[:, 0:8], in_=lgbc[:, 0:8], func=AF.Exp, scale=msc[:, 1:2]),
              reads=["lgbc", "msc"], writes=["kdec"])
        S.act(lambda e: e.activation(out=kdec[:, 8:16], in_=lgbc[:, 8:16], func=AF.Exp, scale=msc[:, 2:3]),
              reads=["lgbc", "msc"], writes=["kdec"])
        S.act(lambda e: e.activation(out=cdec, in_=lgbc, func=AF.Exp, scale=128.0), reads=["lgbc"], writes=["cdec"])
        for h in range(8):
            S.act(lambda e, h=h: e.activation(out=qdt[:, h, :], in_=qeF, func=AF.Exp, scale=lgbc[:, h:h + 1]),
                  reads=["lgbc", "cst"], writes=["qdt"])
            S.act(lambda e, h=h: e.activation(out=qdt[:, 8 + h, :], in_=qeB, func=AF.Exp, scale=lgbc[:, 8 + h:9 + h]),
                  reads=["lgbc", "cst"], writes=["qdt"])
            S.act(lambda e, h=h: e.activation(out=mt1, in_=DF, func=AF.Exp, scale=lgbc[:, h:h + 1]),
                  reads=["lgbc", "cst"], writes=["mt1"])
            S.dve(lambda e: e.tensor_tensor(out=mt1, in0=mt1, in1=allowF, op=ALU.mult), reads=["mt1", "cst"], writes=["mt1"])
            S.act(lambda e, h=h: e.activation(out=mt2, in_=DB, func=AF.Exp, scale=lgbc[:, 8 + h:9 + h]),
                  reads=["lgbc", "cst"], writes=["mt2"])
            S.dve(lambda e: e.tensor_tensor(out=mt2, in0=mt2, in1=allowB, op=ALU.mult), reads=["mt2", "cst"], writes=["mt2"])
            S.dve(lambda e, h=h: e.tensor_tensor(out=maskT[:, h, :], in0=mt1, in1=mt2, op=ALU.add),
                  reads=["mt1", "mt2"], writes=["maskT"])
        S.dve(lambda e: e.tensor_scalar(out=qdt, in0=qdt, scalar1=1.0 / 16, scalar2=None, op0=ALU.mult),
              reads=["qdt"], writes=["qdt"])
        S.dve(lambda e: e.tensor_scalar(out=maskT, in0=maskT, scalar1=1.0 / 16, scalar2=None, op0=ALU.mult),
              reads=["maskT"], writes=["maskT"])

        rawA = AR.alloc([128, 2, NTOK])
        rawB = AR.alloc([128, 2, NTOK])
        k32 = AR.alloc([128, 2, NTOK])
        qT = AR.alloc([128, 2, NTOK], BF16)
        qfT = AR.alloc([128, 2, NTOK], BF16)
        qbT = AR.alloc([128, 2, NTOK], BF16)
        kT = AR.alloc([128, 2, NTOK], BF16)
        v_tok = AR.alloc([128, 16, 256], BF16)
        kf_tok = AR.alloc([128, 16, 256], BF16)
        kb_tok = AR.alloc([128, 16, 256], BF16)
        gate_tok = AR.alloc([128, 16, 256], BF16)
        Sb_all = AR.alloc([128, 16, 2, 256], BF16)
        mix_out = AR.alloc([128, 16, 256])
        Sfp = [AR.alloc([128, 2, 256]) for _ in range(2)]
        Sbp = [AR.alloc([128, 2, 256]) for _ in range(2)]
        cdcc = AR.alloc([128, 16])
        S.dve(lambda e: e.tensor_scalar(out=cdcc, in0=cdec, scalar1=carry, scalar2=None, op0=ALU.mult), reads=["cdec", "msc"], writes=["cdcc"])
        Sf_all = AR.alloc([128, 16, 2, 256], BF16)
        msk = [AR.alloc([128, 128], BF16) for _ in range(2)]
        rt1 = [AR.alloc([128, 512]) for _ in range(2)]
        rt2 = [AR.alloc([128, 512]) for _ in range(2)]
        rsq = AR.alloc([128, 64])

        def rope_apply(raw, rawn, out_bf, out_bfn, out32):
            for half in range(2):
                for tb in range(4):
                    b = nb()
                    i = (half * 4 + tb) % 2
                    blk = raw[:, half, tb * 512:(tb + 1) * 512]
                    S.pe(lambda e, b=b, blk=blk: e.matmul(PS(b), lhsT=perm, rhs=blk, start=True, stop=True),
                         reads=[rawn, "cst"], writes=[PB(b)])
                    if half == 0:
                        cosv = ropeT[:, 8 * tb:8 * tb + 8].unsqueeze(2).to_broadcast([128, 8, 64])
                        sinv = ropeT[:, 32 + 8 * tb:32 + 8 * tb + 8].unsqueeze(2).to_broadcast([128, 8, 64])
                    else:
                        cosv = ropeT[:, 64:128].unsqueeze(1).to_broadcast([128, 8, 64])
                        sinv = ropeT[:, 128:192].unsqueeze(1).to_broadcast([128, 8, 64])
                    v3 = lambda ap: ap.rearrange("p (a b) -> p a b", a=8)
                    S.dve(lambda e, i=i, blk=blk, cosv=cosv: e.tensor_tensor(out=v3(rt1[i]), in0=v3(blk), in1=cosv, op=ALU.mult),
                          reads=[rawn, "ropeT"], writes=["rt1_%d" % i])
                    S.dve(lambda e, i=i, b=b, sinv=sinv: e.tensor_tensor(out=v3(rt2[i]), in0=v3(PS(b)), in1=sinv, op=ALU.mult),
                          reads=["ropeT"], writes=["rt2_%d" % i, PB(b)])
                    if out32 is not None:
                        o32 = out32[:, half, tb * 512:(tb + 1) * 512]
                        S.dve(lambda e, i=i, o32=o32: e.tensor_tensor(out=o32, in0=rt1[i], in1=rt2[i], op=ALU.add),
                               reads=["rt1_%d" % i, "rt2_%d" % i], writes=["k32"])
                        obf = out_bf[:, half, tb * 512:(tb + 1) * 512]
                        S.act(lambda e, o32=o32, obf=obf: e.copy(out=obf, in_=o32), reads=["k32"], writes=[out_bfn])
                    else:
                        obf = out_bf[:, half, tb * 512:(tb + 1) * 512]
                        S.dve(lambda e, i=i, obf=obf: e.tensor_tensor(out=obf, in0=rt1[i], in1=rt2[i], op=ALU.add),
                               reads=["rt1_%d" % i, "rt2_%d" % i], writes=[out_bfn])

        def fm_rows(r0):
            return projT[r0:r0 + 256, :].rearrange("(a p) t -> p a t", p=128)

        for h in range(8):
            S.dma("sp", rawA, fm_rows(h * 256), writes=["rawA"])
            rope_apply(rawA, "rawA", qT, "qT", None)
            c32 = lambda ap: ap.rearrange("p a (c t) -> p (a c) t", t=128)
            S.dve(lambda e, h=h: e.tensor_tensor(out=c32(qfT), in0=c32(qT),
                                                  in1=qdt[:, h, :].unsqueeze(1).to_broadcast([128, 32, 128]), op=ALU.mult),
                   reads=["qT", "qdt"], writes=["qfT"])
            S.dve(lambda e, h=h: e.tensor_tensor(out=c32(qbT), in0=c32(qT),
                                                  in1=qdt[:, 8 + h, :].unsqueeze(1).to_broadcast([128, 32, 128]), op=ALU.mult),
                   reads=["qT", "qdt"], writes=["qbT"])
            S.dma("sp", rawB, fm_rows(2048 + h * 256), writes=["rawB"])
            rope_apply(rawB, "rawB", kT, "kT", k32)
            for tt in range(16):
                b = nb()
                for half in range(2):
                    S.pe(lambda e, b=b, half=half, tt=tt: e.transpose(out=PS(b, half * 128, (half + 1) * 128),
                                                                       in_=k32[:, half, tt * 128:(tt + 1) * 128], identity=ident),
                         reads=["k32", "cst"], writes=[PB(b)])
                S.act(lambda e, b=b, tt=tt, h=h: e.activation(out=kf_tok[:, tt, :], in_=PS(b, 0, 256), func=AF.Copy,
                                                              scale=kdec[:, h:h + 1]),
                      reads=["kdec"], writes=["kf_tok", PB(b)])
                S.dve(lambda e, b=b, tt=tt, h=h: e.tensor_scalar(out=kb_tok[:, tt, :], in0=PS(b, 0, 256),
                                                                 scalar1=kdec[:, 8 + h:9 + h], scalar2=None, op0=ALU.mult),
                      reads=["kdec"], writes=["kb_tok", PB(b)])
            S.dma("sp", rawA, fm_rows(4096 + h * 256), writes=["rawA"])
            for tt in range(16):
                b = nb()
                for half in range(2):
                    S.pe(lambda e, b=b, half=half, tt=tt: e.transpose(out=PS(b, half * 128, (half + 1) * 128),
                                                                       in_=rawA[:, half, tt * 128:(tt + 1) * 128], identity=ident),
                         reads=["rawA", "cst"], writes=[PB(b)])
                S.act(lambda e, b=b, tt=tt: e.copy(out=v_tok[:, tt, :], in_=PS(b, 0, 256)), writes=["v_tok", PB(b)])
            S.dma("sp", rawB, fm_rows(6144 + h * 256), writes=["rawB"])
            S.act(lambda e: e.activation(out=rawB, in_=rawB, func=AF.Silu), reads=["rawB"], writes=["rawB"])
            for tt in range(16):
                b = nb()
                for half in range(2):
                    S.pe(lambda e, b=b, half=half, tt=tt: e.transpose(out=PS(b, half * 128, (half + 1) * 128),
                                                                       in_=rawB[:, half, tt * 128:(tt + 1) * 128], identity=ident),
                         reads=["rawB", "cst"], writes=[PB(b)])
                S.dve(lambda e, b=b, tt=tt: e.tensor_copy(out=gate_tok[:, tt, :], in_=PS(b, 0, 256)), writes=["gate_tok", PB(b)])
            S.dma("sp", Sbp[0], s_rb[h].rearrange("(a p) v -> p a v", p=128), writes=["Sb_0"])
            S.dma("sp", Sfp[0], s_rf[h].rearrange("(a p) v -> p a v", p=128), writes=["Sf_0"])
            f2 = lambda ap: ap.rearrange("p a v -> p (a v)")
            for i in range(16):
                for dr_ in (1, 0):
                    n = 15 - i if dr_ == 1 else i
                    Sp, pre, Sall, San = (Sbp, "Sb_", Sb_all, "Sb_all") if dr_ == 1 else (Sfp, "Sf_", Sf_all, "Sf_all")
                    cur, nxt = Sp[i % 2], Sp[(i + 1) % 2]
                    curn, nxtn = pre + str(i % 2), pre + str((i + 1) % 2)
                    ktk, ktn = (kb_tok, "kb_tok") if dr_ == 1 else (kf_tok, "kf_tok")
                    col = 8 + h if dr_ == 1 else h
                    bprev = (n % 2 == 1 and n < 15) if dr_ == 1 else (n % 2 == 0 and n > 0)
                    if bprev:
                        S.act(lambda e, n=n, Sall=Sall, cur=cur: e.activation(out=Sall[:, n], in_=cur, func=AF.Copy, scale=carry),
                              reads=[curn, "msc"], writes=[San])
                    else:
                        S.act(lambda e, n=n, Sall=Sall, cur=cur: e.copy(out=Sall[:, n], in_=cur), reads=[curn], writes=[San])
                    b = nb()
                    for half in range(2):
                        S.pe(lambda e, b=b, half=half, n=n, ktk=ktk: e.matmul(PS(b, half * 256, (half + 1) * 256),
                                                                              lhsT=ktk[:, n, half * 128:(half + 1) * 128], rhs=v_tok[:, n, :],
                                                                              start=True, stop=True),
                             reads=[ktn, "v_tok"], writes=[PB(b)])
                    cd = (cdcc if bprev else cdec)[:, col:col + 1]
                    S.dve(lambda e, b=b, cur=cur, nxt=nxt, cd=cd: e.scalar_tensor_tensor(out=f2(nxt), in0=f2(cur), scalar=cd, in1=PS(b),
                                                                                         op0=ALU.mult, op1=ALU.add),
                          reads=["cdec", "cdcc", curn], writes=[nxtn, PB(b)])
                    seg_end = (n % 2 == 0) if dr_ == 1 else (n % 2 == 1)
                    if seg_end:
                        od = o_rb if dr_ == 1 else o_rf
                        S.dma("pool", od[n // 2, h].rearrange("(a p) v -> p a v", p=128), nxt, reads=[nxtn], is_out=True)
            def sc_stage(n, h=h):
                cs = slice(n * 128, (n + 1) * 128)
                b1 = nb()
                for half in range(2):
                    S.pe(lambda e, b1=b1, half=half, cs=cs: e.matmul(PS(b1, 0, 128), lhsT=kT[:, half, cs], rhs=qT[:, half, cs],
                                                                    start=(half == 0), stop=(half == 1)),
                         reads=["kT", "qT"], writes=[PB(b1)])
                mk = msk[n % 2]
                mkn = "msk%d" % (n % 2)
                S.dve(lambda e, b1=b1, mk=mk: e.tensor_tensor(out=mk, in0=PS(b1, 0, 128), in1=maskT[:, h, :], op=ALU.mult),
                      reads=["maskT"], writes=[mkn, PB(b1)])

            def o_stage(n, h=h):
                cs = slice(n * 128, (n + 1) * 128)
                mk = msk[n % 2]
                mkn = "msk%d" % (n % 2)
                b2 = nb()
                S.pe(lambda e, b2=b2, mk=mk, n=n: e.matmul(PS(b2, 0, 256), lhsT=mk, rhs=v_tok[:, n, :], start=True, stop=False),
                     reads=[mkn, "v_tok"], writes=[PB(b2)])
                for half in range(2):
                    S.pe(lambda e, b2=b2, half=half, cs=cs, n=n: e.matmul(PS(b2, 0, 256), lhsT=qfT[:, half, cs], rhs=Sf_all[:, n, half, :],
                                                                         start=False, stop=False),
                         reads=["qfT", "Sf_all"], writes=[PB(b2)])
                for half in range(2):
                    S.pe(lambda e, b2=b2, half=half, cs=cs, n=n: e.matmul(PS(b2, 0, 256), lhsT=qbT[:, half, cs],
                                                                         rhs=Sb_all[:, n, half, :], start=False, stop=(half == 1)),
                         reads=["qbT", "Sb_all"], writes=[PB(b2)])
                S.act(lambda e, b2=b2, n=n: e.activation(out=rt1[0][:, 0:256], in_=PS(b2, 0, 256), func=AF.Square,
                                                         accum_out=rsq[:, n:n + 1]),
                      writes=["rt1_0", "rsq", PB(b2)])
                S.act(lambda e, n=n: e.activation(out=rsq[:, 16 + n:17 + n], in_=rsq[:, n:n + 1], func=AF.Sqrt,
                                                  scale=1.0 / 256, bias=EPS), reads=["rsq"], writes=["rsq"])
                S.dve(lambda e, n=n: e.reciprocal(out=rsq[:, 32 + n:33 + n], in_=rsq[:, 16 + n:17 + n]), reads=["rsq"], writes=["rsq"])
                S.dve(lambda e, b2=b2, n=n: e.scalar_tensor_tensor(out=mix_out[:, n, :], in0=PS(b2, 0, 256),
                                                                   scalar=rsq[:, 32 + n:33 + n], in1=gate_tok[:, n, :],
                                                                   op0=ALU.mult, op1=ALU.mult),
                      reads=["rsq", "gate_tok"], writes=["mix_out", PB(b2)])

            sc_stage(0)
            for n in range(16):
                if n + 1 < 16:
                    sc_stage(n + 1)
                o_stage(n)
            S.dma("pool", mixed[:, h * 256:(h + 1) * 256].rearrange("(tt p) c -> p tt c", p=128), mix_out,
                  reads=["mix_out"], writes=["mixed"])
        S.barrier(bar[:, 0:1])

        if debug is not None and "stop2" in debug:
            S.finish()
            S.emit(nc, st)
            return nc

        AR.top = persist_top
        gbr = AR.alloc([128, NTOK])
        gp = AR.alloc([128, 4])
        S.dma("sp", gp[0:32, 0:2], gpar, writes=["gp"])
        S.dma("sp", gbr[0:64, :], projT[16384:16448, :], writes=["gbr"])
        S.act(lambda e: e.activation(out=gp[0:32, 2:3], in_=gp[0:32, 0:1], func=AF.Exp), reads=["gp"], writes=["gp"])
        S.dve(lambda e: e.tensor_scalar(out=gp[0:32, 3:4], in0=gp[0:32, 2:3], scalar1=-1.0, scalar2=None, op0=ALU.mult),
              reads=["gp"], writes=["gp"])
        S.act(lambda e: e.activation(out=gbr[0:32, :], in_=gbr[0:32, :], func=AF.Exp, bias=gp[0:32, 1:2]),
              reads=["gbr", "gp"], writes=["gbr"])
        S.act(lambda e: e.activation(out=gbr[0:32, :], in_=gbr[0:32, :], func=AF.Ln, bias=1.0), reads=["gbr"], writes=["gbr"])
        S.dve(lambda e: e.tensor_scalar(out=gbr[0:32, :], in0=gbr[0:32, :], scalar1=gp[0:32, 3:4], scalar2=None, op0=ALU.mult),
              reads=["gbr", "gp"], writes=["gbr"])
        S.act(lambda e: e.activation(out=gbr[32:64, :], in_=gbr[32:64, :], func=AF.Sigmoid), reads=["gbr"], writes=["gbr"])
        for tt in range(16):
            b = nb()
            S.pe(lambda e, b=b, tt=tt: e.transpose(out=PS(b, 0, 64), in_=gbr[0:64, tt * 128:(tt + 1) * 128], identity=ident[0:64, 0:64]),
                 reads=["gbr", "cst"], writes=[PB(b)])
            S.act(lambda e, b=b, tt=tt: e.copy(out=gb_tok[:, tt, :], in_=PS(b, 0, 64)), writes=["gb_tok", PB(b)])

        cw = AR.alloc([128, 48, 5])
        cwc = AR.alloc([128, 48, 5])
        S.dma("sp", cw, convw, writes=["cw"])
        S.dve(lambda e: e.tensor_scalar(out=cwc, in0=cw, scalar1=carry, scalar2=None, op0=ALU.mult), reads=["cw", "msc"], writes=["cwc"])
        xr = [AR.alloc([128, NTOK]) for _ in range(2)]
        acc2 = [None, None]
        xp2 = [AR.alloc([128, 8, 260], BF16) for _ in range(2)]
        dg2 = [AR.alloc([128, 5, 128], BF16) for _ in range(2)]
        for i_ in range(2):
            S.dve(lambda e, i_=i_: e.memset(xp2[i_], 0.0), writes=["xp%d" % i_])
        yv2 = [AR.alloc([128, NTOK]) for _ in range(2)]
        sqv2 = [AR.alloc([128, NTOK]) for _ in range(2)]
        rnv2 = [AR.alloc([128, NTOK]) for _ in range(2)]
        obf = [AR.alloc([128, NTOK], BF16) for _ in range(2)]
        tokst = [AR.alloc([128, 16, 128]) for _ in range(2)]
        v3s = lambda ap: ap.rearrange("p (s t) -> p s t", t=256)
        cnt3 = [0]
        def bufs3(g):
            i = g % 2
            return (acc2[i], yv2[i], sqv2[i], rnv2[i], xr[i], "acc%d" % i, "yv%d" % i, "sqv%d" % i, "rnv%d" % i, "xr%d" % i)

        def stageX(g):
            conv = g < 48
            acc, yv, sqv, rnv, xt_, accn, yvn, sqvn, rnvn, xn_ = bufs3(g)
            i = g % 2
            r0 = 8192 + g * 128 if conv else 14336 + (g - 48) * 128
            S.dma("sp", xt_, projT[r0:r0 + 128, :], writes=[xn_])
            if conv:
                xp_, dg_ = xp2[i], dg2[i]
                xpn, dgn_ = "xp%d" % i, "dg%d" % i
                S.act(lambda e: e.copy(out=xp_[:, :, 2:258], in_=v3s(xt_)), reads=[xn_], writes=[xpn])
                S.dve(lambda e: e.tensor_scalar(out=xp_[:, 1:8, 0:2], in0=v3s(xt_)[:, 0:7, 254:256], scalar1=carry, scalar2=None, op0=ALU.mult),
                      reads=[xn_, "msc"], writes=[xpn])
                S.dve(lambda e: e.tensor_scalar(out=xp_[:, 0:7, 258:260], in0=v3s(xt_)[:, 1:8, 0:2], scalar1=carry, scalar2=None, op0=ALU.mult),
                      reads=[xn_, "msc"], writes=[xpn])
                S.pool(lambda e: e.tensor_tensor(out=dg_, in0=ident.unsqueeze(1).to_broadcast([128, 5, 128]),
                                                 in1=cw[:, g, :].unsqueeze(2).to_broadcast([128, 5, 128]), op=ALU.mult),
                       reads=["cst", "cw"], writes=[dgn_])
                for q4 in range(4):
                    b = nb()
                    for sh in range(2):
                        s_ = q4 * 2 + sh
                        for tap in range(5):
                            S.pe(lambda e, b=b, sh=sh, s_=s_, tap=tap: e.matmul(PS(b, sh * 256, (sh + 1) * 256), lhsT=dg_[:, tap, :],
                                                                               rhs=xp_[:, s_, tap:tap + 256], start=(tap == 0), stop=(tap == 4)),
                                 reads=[xpn, dgn_], writes=[PB(b)])
                    S.act(lambda e, b=b, q4=q4: e.activation(out=yv[:, q4 * 512:(q4 + 1) * 512], in_=PS(b), func=AF.Silu),
                          writes=[yvn, PB(b)])
            else:
                S.act(lambda e: e.activation(out=yv, in_=xt_, func=AF.Silu), reads=[xn_], writes=[yvn])
            if g < 32:
                S.pool(lambda e: e.tensor_tensor(out=sqv, in0=yv, in1=yv, op=ALU.mult), reads=[yvn], writes=[sqvn])
                for tb in range(4):
                    b = nb()
                    S.pe(lambda e, b=b, tb=tb: e.matmul(PS(b), lhsT=ones, rhs=sqv[:, tb * 512:(tb + 1) * 512], start=True, stop=True),
                         reads=[sqvn, "ones"], writes=[PB(b)])
                    S.act(lambda e, b=b, tb=tb: e.activation(out=rnv[:, tb * 512:(tb + 1) * 512], in_=PS(b), func=AF.Sqrt, bias=EPS),
                          writes=[rnvn, PB(b)])

        def stageY(g):
            acc, yv, sqv, rnv, xt_, accn, yvn, sqvn, rnvn, xn_ = bufs3(g)
            if g < 32:
                S.dve(lambda e: e.reciprocal(out=rnv, in_=rnv), reads=[rnvn], writes=[rnvn])
                scl = (128.0 ** -0.5) if g < 16 else 1.0
                S.dve(lambda e: e.scalar_tensor_tensor(out=yv, in0=yv, scalar=scl, in1=rnv, op0=ALU.mult, op1=ALU.mult),
                      reads=[yvn, rnvn], writes=[yvn])
                ob_ = obf[g % 2]
                obn = "obf%d" % (g % 2)
                S.pool(lambda e: e.tensor_copy(out=ob_, in_=yv), reads=[yvn], writes=[obn])
                dst = gqT[g] if g < 16 else gkT[g - 16]
                S.dma("act", dst, ob_, reads=[obn], writes=["gqkT"])
            if g >= 16:
                hh = (g - 16) % 16
                dram = gk_tok if g < 32 else (gv_tok if g < 48 else gz_tok)
                ts_ = tokst[cnt3[0] % 2]
                tsn = "tokst%d" % (cnt3[0] % 2)
                cnt3[0] += 1
                for q4 in range(4):
                    b = nb()
                    for j in range(4):
                        tt = q4 * 4 + j
                        S.pe(lambda e, b=b, j=j, tt=tt: e.transpose(out=PS(b, j * 128, (j + 1) * 128), in_=yv[:, tt * 128:(tt + 1) * 128],
                                                                     identity=ident), reads=[yvn, "cst"], writes=[PB(b)])
                    if q4 % 2 == 0:
                        S.act(lambda e, b=b, q4=q4: e.copy(out=ts_[:, q4 * 4:q4 * 4 + 4, :].rearrange("p a b -> p (a b)"), in_=PS(b)),
                              writes=[tsn, PB(b)])
                    else:
                        S.dve(lambda e, b=b, q4=q4: e.tensor_copy(out=ts_[:, q4 * 4:q4 * 4 + 4, :].rearrange("p a b -> p (a b)"), in_=PS(b)),
                              writes=[tsn, PB(b)])
                S.dma("act", dram[:, hh * 128:(hh + 1) * 128].rearrange("(tt p) c -> p tt c", p=128), ts_, reads=[tsn], writes=["gtok"])

        stageX(0)
        for g in range(64):
            if g + 1 < 64:
                stageX(g + 1)
            stageY(g)
        S.barrier(bar[:, 0:1])
        if debug is not None and "stop3a" in debug:
            S.dma("sp", y[0:128, 0:1024], gb_tok.rearrange("p a b -> p (a b)"), is_out=True)
            S.finish()
            S.emit(nc, st)
            return nc

        S = S_real
        if debug is not None and "only3b" in debug:
            S.dve(lambda e: e.memset(gb_tok, -0.05), writes=["gb_tok"])
        AR.top = persist_top
        gnwb = AR.alloc([128, 128])
        S.dma("sp", gnwb, gnw[0:1, :].to_broadcast([128, 128]), writes=["gnwb"])
        Sg32 = AR.alloc([128, 16, 128])
        Sgbf = AR.alloc([128, 16, 128], BF16)
        sgst = AR.alloc([128, 16, 128])
        qTc = [AR.alloc([128, 16, 128], BF16) for _ in range(2)]
        kTc = [AR.alloc([128, 16, 128], BF16) for _ in range(2)]
        ktok1 = AR.alloc([128, 16, 128])
        ktok = [ktok1, ktok1]
        QTf = AR.alloc([128, 16, 128])
        Mtmp = AR.alloc([128, 16, 128])
        vtok = [AR.alloc([128, 16, 128]) for _ in range(2)]
        oacc = [AR.alloc([128, 16, 128]) for _ in range(2)]
        obl = AR.alloc([128, 16, 128])
        gzt = AR.alloc([128, 16, 128])
        GT = AR.alloc([128, 16, 128])
        Qm = AR.alloc([128, 16, 128])
        QTm = AR.alloc([128, 16, 128])
        Yv = AR.alloc([128, 16, 128])
        t1v = AR.alloc([128, 16, 128])
        oBs = AR.alloc([128, 16, 128])
        PTv = [AR.alloc([128, 16, 128], BF16) for _ in range(2)]
        TpT = [AR.alloc([128, 16, 128], BF16) for _ in range(2)]
        ktd = [AR.alloc([128, 16, 128], BF16) for _ in range(2)]
        rhs0 = AR.alloc([128, 16, 128], BF16)
        vnew = AR.alloc([128, 16, 128], BF16)
        gcv = [AR.alloc([128, 160]) for _ in range(2)]
        gsm = AR.alloc([128, 64])
        fl = lambda ap: ap.rearrange("p a b -> p (a b)")
        sl4 = lambda ap, hg: ap[:, 4 * hg:4 * hg + 4, :].rearrange("p a b -> p (a b)")

        def load_chunk(c, par):
            cs = slice(c * 128, (c + 1) * 128)
            S.dma("sp", qTc[par], gqT[:, :, cs].rearrange("h p t -> p h t"), writes=["qTc%d" % par])
            S.dma("sp", kTc[par], gkT[:, :, cs].rearrange("h p t -> p h t"), writes=["kTc%d" % par])
            S.dma("sp", fl(ktok[par]), gk_tok[cs, :], writes=["ktok"])
            S.dma("sp", fl(vtok[par]), gv_tok[cs, :], writes=["vtok%d" % par])

        def pipe2(first, second):
            for hg in range(4):
                first(hg)
                if hg >= 1:
                    second(hg - 1)
            second(3)

        def stageA(c, par, dr):
            tri = triF if dr == 0 else triB
            negm = negmF if dr == 0 else negmB
            strict = strictF if dr == 0 else strictB
            G = gb_tok[:, c, dr * 16:(dr + 1) * 16]
            Bt = gb_tok[:, c, 32 + dr * 16:32 + (dr + 1) * 16]
            gv_ = gcv[par]
            gn = "gcv%d" % par
            b = nb()
            S.pe(lambda e, b=b: e.matmul(PS(b, 0, 16), lhsT=tri, rhs=G, start=True, stop=True), reads=["gb_tok", "cst"], writes=[PB(b)])
            S.pe(lambda e, b=b: e.matmul(PS(b, 16, 32), lhsT=ones, rhs=G, start=True, stop=True), reads=["gb_tok", "ones"], writes=[PB(b)])
            S.dve(lambda e, b=b: e.tensor_copy(out=gv_[:, 0:32], in_=PS(b, 0, 32)), writes=[gn, PB(b)])
            S.act(lambda e: e.activation(out=gv_[:, 32:48], in_=gv_[:, 0:16], func=AF.Exp), reads=[gn], writes=[gn])
            S.dve(lambda e: e.tensor_scalar(out=gv_[:, 48:64], in0=gv_[:, 32:48], scalar1=-1.0, scalar2=None, op0=ALU.mult), reads=[gn], writes=[gn])
            S.dve(lambda e: e.tensor_tensor(out=gv_[:, 112:128], in0=gv_[:, 16:32], in1=gv_[:, 0:16], op=ALU.subtract), reads=[gn], writes=[gn])
            S.act(lambda e: e.activation(out=gv_[:, 64:80], in_=gv_[:, 112:128], func=AF.Exp), reads=[gn], writes=[gn])
            S.act(lambda e: e.activation(out=gv_[:, 80:96], in_=gv_[:, 16:32], func=AF.Exp), reads=[gn], writes=[gn])
            S.dve(lambda e: e.tensor_scalar(out=gv_[:, 96:112], in0=Bt, scalar1=-1.0, scalar2=None, op0=ALU.mult), reads=["gb_tok"], writes=[gn])
            for h in range(16):
                S.act(lambda e, h=h: e.activation(out=ktd[par][:, h, :], in_=ktok[par][:, h, :], func=AF.Copy, scale=gv_[:, 64 + h:65 + h]),
                      reads=[gn, "ktok"], writes=["ktd%d" % par])
            yield
            for hg in range(4):
                b = nb()
                for j in range(4):
                    h = 4 * hg + j
                    S.pe(lambda e, b=b, j=j, h=h: e.matmul(PS(b, j * 128, (j + 1) * 128), lhsT=G[:, h:h + 1].to_broadcast([128, 128]), rhs=tri,
                                                           start=True, stop=True), reads=["gb_tok", "cst"], writes=[PB(b)])
                for j in range(4):
                    h = 4 * hg + j
                    S.dve(lambda e, b=b, j=j, h=h: e.scalar_tensor_tensor(out=GT[:, h, :], in0=PS(b, j * 128, (j + 1) * 128), scalar=gv_[:, h:h + 1],
                                                                         in1=negm, op0=ALU.subtract, op1=ALU.add),
                          reads=[gn, "cst"], writes=["GT%d" % hg, PB(b)])
                S.act(lambda e, hg=hg: e.activation(out=sl4(GT, hg), in_=sl4(GT, hg), func=AF.Exp), reads=["GT%d" % hg], writes=["GT%d" % hg])
            yield
            for hg in range(4):
                b2, b3 = nb(), nb()
                for j in range(4):
                    h = 4 * hg + j
                    S.pe(lambda e, b2=b2, j=j, h=h: e.matmul(PS(b2, j * 128, (j + 1) * 128), lhsT=kTc[par][:, h, :], rhs=kTc[par][:, h, :],
                                                             start=True, stop=True), reads=["kTc%d" % par], writes=[PB(b2)])
                for j in range(4):
                    h = 4 * hg + j
                    S.pe(lambda e, b3=b3, j=j, h=h: e.matmul(PS(b3, j * 128, (j + 1) * 128), lhsT=kTc[par][:, h, :], rhs=qTc[par][:, h, :],
                                                             start=True, stop=True), reads=["kTc%d" % par, "qTc%d" % par], writes=[PB(b3)])
                S.dve(lambda e, b2=b2, hg=hg: e.tensor_tensor(out=sl4(t1v, hg), in0=PS(b2), in1=sl4(GT, hg), op=ALU.mult),
                      reads=["GT%d" % hg], writes=["t1v%d" % hg, PB(b2)])
                for j in range(4):
                    h = 4 * hg + j
                    S.dve(lambda e, h=h: e.scalar_tensor_tensor(out=Qm[:, h, :], in0=t1v[:, h, :], scalar=gv_[:, 96 + h:97 + h], in1=strict,
                                                                op0=ALU.mult, op1=ALU.mult),
                          reads=["t1v%d" % hg, gn, "cst"], writes=["Qm%d" % hg])
                S.dve(lambda e, b3=b3, hg=hg: e.tensor_tensor(out=sl4(PTv[par], hg), in0=PS(b3), in1=sl4(GT, hg), op=ALU.mult),
                      reads=["GT%d" % hg], writes=["PT%d_%d" % (par, hg), PB(b3)])
            yield
            bd16 = CT(14)
            mxs = [CT(15 + 2 * k + (1 - dr)) for k in range(3)]
            bc4 = lambda t: t.unsqueeze(1).to_broadcast([128, 4, 128])
            s4 = lambda ap, hg: ap[:, 4 * hg:4 * hg + 4, :]
            for hg in range(4):
                b4 = nb()
                for j in range(4):
                    h = 4 * hg + j
                    S.pe(lambda e, b4=b4, j=j, h=h: e.transpose(out=PS(b4, j * 128, (j + 1) * 128), in_=Qm[:, h, :], identity=ident),
                         reads=["Qm%d" % hg, "cst"], writes=[PB(b4)])
                S.act(lambda e, b4=b4, hg=hg: e.copy(out=sl4(QTf, hg), in_=PS(b4)), writes=["QTf%d" % hg, PB(b4)])
                S.pool(lambda e, hg=hg: e.tensor_tensor(out=s4(QTm, hg), in0=s4(QTf, hg), in1=bc4(bd16), op=ALU.mult),
                       reads=["QTf%d" % hg, "cst"], writes=["QTm%d" % hg])
                S.pool(lambda e, hg=hg: e.tensor_tensor(out=s4(Qm, hg), in0=s4(Qm, hg), in1=bc4(bd16), op=ALU.mult),
                       reads=["Qm%d" % hg, "cst"], writes=["Qm%d" % hg])
                S.pool(lambda e, hg=hg: e.tensor_tensor(out=s4(Yv, hg), in0=s4(Qm, hg), in1=bc4(ident), op=ALU.add),
                       reads=["Qm%d" % hg, "cst"], writes=["Yv%d" % hg])
            yield
            def lv_first(hg, m):
                b5 = nb()
                for j in range(4):
                    h = 4 * hg + j
                    S.pe(lambda e, b5=b5, j=j, h=h: e.matmul(PS(b5, j * 128, (j + 1) * 128), lhsT=Qm[:, h, :], rhs=QTm[:, h, :],
                                                             start=True, stop=True), reads=["Qm%d" % hg, "QTm%d" % hg], writes=[PB(b5)])
                if m < 3:
                    b6 = nb()
                    for j in range(4):
                        h = 4 * hg + j
                        S.pe(lambda e, b6=b6, j=j, h=h: e.matmul(PS(b6, j * 128, (j + 1) * 128), lhsT=QTm[:, h, :], rhs=Qm[:, h, :],
                                                                 start=True, stop=True), reads=["Qm%d" % hg, "QTm%d" % hg], writes=[PB(b6)])
                S.act(lambda e, b5=b5, hg=hg: e.copy(out=sl4(QTm, hg), in_=PS(b5)), writes=["QTm%d" % hg, PB(b5)])
                if m < 3:
                    S.dve(lambda e, b6=b6, hg=hg: e.tensor_copy(out=sl4(Qm, hg), in_=PS(b6)), writes=["Qm%d" % hg, PB(b6)])

            def lv_second(hg):
                b7 = nb()
                for j in range(4):
                    h = 4 * hg + j
                    S.pe(lambda e, b7=b7, j=j, h=h: e.matmul(PS(b7, j * 128, (j + 1) * 128), lhsT=QTm[:, h, :], rhs=Yv[:, h, :],
                                                             start=True, stop=True), reads=["QTm%d" % hg, "Yv%d" % hg], writes=[PB(b7)])
                S.dve(lambda e, b7=b7, hg=hg: e.tensor_tensor(out=sl4(Yv, hg), in0=PS(b7), in1=sl4(Yv, hg), op=ALU.add),
                      reads=["Yv%d" % hg], writes=["Yv%d" % hg, PB(b7)])

            for m in range(1, 4):
                for hg in range(4):
                    lv_first(hg, m)
                    if hg >= 2:
                        lv_second(hg - 2)
                lv_second(2)
                lv_second(3)
                yield
            for hg in range(4):
                b8 = nb()
                for j in range(4):
                    h = 4 * hg + j
                    S.pe(lambda e, b8=b8, j=j, h=h: e.transpose(out=PS(b8, j * 128, (j + 1) * 128), in_=Yv[:, h, :], identity=ident),
                         reads=["Yv%d" % hg, "cst"], writes=[PB(b8)])
                S.act(lambda e, b8=b8, hg=hg: e.copy(out=sl4(GT, hg), in_=PS(b8)), writes=["GT%d" % hg, PB(b8)])
            yield
            for k in range(3):
                def mg_first(hg, k=k):
                    if k == 0:
                        S.pool(lambda e, hg=hg, k=k: e.tensor_tensor(out=s4(Mtmp, hg), in0=s4(QTf, hg), in1=bc4(mxs[k]), op=ALU.mult),
                               reads=["QTf%d" % hg, "cst"], writes=["Mtmp%d" % hg])
                    bz = nb()
                    for j in range(4):
                        h = 4 * hg + j
                        S.pe(lambda e, bz=bz, j=j, h=h: e.matmul(PS(bz, j * 128, (j + 1) * 128), lhsT=Mtmp[:, h, :], rhs=Yv[:, h, :],
                                                                 start=True, stop=True), reads=["Mtmp%d" % hg, "Yv%d" % hg], writes=[PB(bz)])
                    S.act(lambda e, bz=bz, hg=hg: e.copy(out=sl4(t1v, hg), in_=PS(bz)), writes=["t1v%d" % hg, PB(bz)])

                def mg_second(hg, k=k):
                    bw = nb()
                    for j in range(4):
                        h = 4 * hg + j
                        S.pe(lambda e, bw=bw, j=j, h=h: e.matmul(PS(bw, j * 128, (j + 1) * 128), lhsT=GT[:, h, :], rhs=t1v[:, h, :],
                                                                 start=True, stop=True), reads=["GT%d" % hg, "t1v%d" % hg], writes=[PB(bw)])
                    if k < 2:
                        bt = nb()
                        for j in range(4):
                            h = 4 * hg + j
                            S.pe(lambda e, bt=bt, j=j, h=h: e.matmul(PS(bt, j * 128, (j + 1) * 128), lhsT=t1v[:, h, :], rhs=GT[:, h, :],
                                                                     start=True, stop=True), reads=["GT%d" % hg, "t1v%d" % hg], writes=[PB(bt)])
                    S.dve(lambda e, bw=bw, hg=hg: e.tensor_tensor(out=sl4(Yv, hg), in0=PS(bw), in1=sl4(Yv, hg), op=ALU.add),
                          reads=["Yv%d" % hg], writes=["Yv%d" % hg, PB(bw)])
                    if k < 2:
                        S.dve(lambda e, bt=bt, hg=hg: e.tensor_tensor(out=sl4(GT, hg), in0=PS(bt), in1=sl4(GT, hg), op=ALU.add),
                              reads=["GT%d" % hg], writes=["GT%d" % hg, PB(bt)])
                        S.pool(lambda e, hg=hg, k=k: e.tensor_tensor(out=s4(Mtmp, hg), in0=s4(QTf, hg), in1=bc4(mxs[k + 1]), op=ALU.mult),
                               reads=["QTf%d" % hg, "cst"], writes=["Mtmp%d" % hg])
                for hg in range(4):
                    mg_first(hg)
                    if hg >= 2:
                        mg_second(hg - 2)
                mg_second(2)
                mg_second(3)
                yield
            for hg in range(4):
                S.act(lambda e, hg=hg: e.copy(out=sl4(TpT[par], hg), in_=sl4(Yv, hg)), reads=["Yv%d" % hg], writes=["TpT%d_%d" % (par, hg)])
            yield

        def stageB(c, par, dr):
            gv_ = gcv[par]
            gn = "gcv%d" % par
            Bt = gb_tok[:, c, 32 + dr * 16:32 + (dr + 1) * 16]
            bk = {}
            for hg in range(4):
                b = bk[hg] = nb()
                for j in range(4):
                    h = 4 * hg + j
                    S.pe(lambda e, b=b, j=j, h=h: e.matmul(PS(b, j * 128, (j + 1) * 128), lhsT=kTc[par][:, h, :], rhs=Sgbf[:, h, :],
                                                           start=True, stop=True), reads=["kTc%d" % par, "Sgbf%d" % hg], writes=[PB(b)])
                for j in range(4):
                    h = 4 * hg + j
                    S.dve(lambda e, b=b, j=j, h=h: e.scalar_tensor_tensor(out=rhs0[:, h, :], in0=PS(b, j * 128, (j + 1) * 128),
                                                                         scalar=gv_[:, 48 + h:49 + h], in1=vtok[par][:, h, :],
                                                                         op0=ALU.mult, op1=ALU.add),
                          reads=[gn, "vtok%d" % par], writes=["rhs0_%d" % hg, PB(b)])
            yield
            for hg in range(4):
                b = nb()
                for j in range(4):
                    h = 4 * hg + j
                    S.pe(lambda e, b=b, j=j, h=h: e.matmul(PS(b, j * 128, (j + 1) * 128), lhsT=TpT[par][:, h, :], rhs=rhs0[:, h, :],
                                                           start=True, stop=True), reads=["TpT%d_%d" % (par, hg), "rhs0_%d" % hg], writes=[PB(b)])
                for j in range(4):
                    h = 4 * hg + j
                    S.act(lambda e, b=b, j=j, h=h: e.activation(out=vnew[:, h, :], in_=PS(b, j * 128, (j + 1) * 128), func=AF.Copy,
                                                                scale=Bt[:, h:h + 1]),
                          reads=["gb_tok"], writes=["vnew%d" % hg, PB(b)])
            yield
            for hg in range(4):
                bA, bB = nb(), nb()
                for j in range(4):
                    h = 4 * hg + j
                    S.pe(lambda e, bA=bA, j=j, h=h: e.matmul(PS(bA, j * 128, (j + 1) * 128), lhsT=qTc[par][:, h, :], rhs=Sgbf[:, h, :],
                                                             start=True, stop=True), reads=["qTc%d" % par, "Sgbf%d" % hg], writes=[PB(bA)])
                for j in range(4):
                    h = 4 * hg + j
                    S.pe(lambda e, bB=bB, j=j, h=h: e.matmul(PS(bB, j * 128, (j + 1) * 128), lhsT=PTv[par][:, h, :], rhs=vnew[:, h, :],
                                                             start=True, stop=True), reads=["PT%d_%d" % (par, hg), "vnew%d" % hg], writes=[PB(bB)])
                S.act(lambda e, bB=bB, hg=hg: e.copy(out=sl4(oBs, hg), in_=PS(bB)), writes=["oBs%d" % hg, PB(bB)])
                for j in range(4):
                    h = 4 * hg + j
                    S.dve(lambda e, bA=bA, j=j, h=h: e.scalar_tensor_tensor(out=oacc[par][:, h, :], in0=PS(bA, j * 128, (j + 1) * 128),
                                                                           scalar=gv_[:, 32 + h:33 + h], in1=oBs[:, h, :],
                                                                           op0=ALU.mult, op1=ALU.add),
                          reads=[gn, "oBs%d" % hg], writes=["oacc%d" % par, PB(bA)])
            yield
            for hg in range(4):
                bS = nb()
                for j in range(4):
                    h = 4 * hg + j
                    S.pe(lambda e, bS=bS, j=j, h=h: e.matmul(PS(bS, j * 128, (j + 1) * 128), lhsT=ktd[par][:, h, :], rhs=vnew[:, h, :],
                                                             start=True, stop=True), reads=["ktd%d" % par, "vnew%d" % hg], writes=[PB(bS)])
                for j in range(4):
                    h = 4 * hg + j
                    S.dve(lambda e, bS=bS, j=j, h=h: e.scalar_tensor_tensor(out=Sg32[:, h, :], in0=Sg32[:, h, :], scalar=gv_[:, 80 + h:81 + h],
                                                                           in1=PS(bS, j * 128, (j + 1) * 128), op0=ALU.mult, op1=ALU.add),
                          reads=[gn], writes=["Sg32_%d" % hg, PB(bS)])
            yield
            seg_end = (c % 2 == 1) if dr == 0 else (c % 2 == 0)
            last = (c == 15) if dr == 0 else (c == 0)
            allS = ["Sg32_%d" % hg for hg in range(4)]
            if seg_end:
                S.act(lambda e: e.copy(out=sgst, in_=Sg32), reads=allS, writes=["sgst"])
                dst = (o_gf if dr == 0 else o_gb)[c // 2].rearrange("h p v -> p h v")
                S.dma("pool", dst, sgst, reads=["sgst"], is_out=True)
                if not last:
                    S.dve(lambda e: e.tensor_scalar(out=Sg32, in0=Sg32, scalar1=carry, scalar2=None, op0=ALU.mult),
                          reads=allS + ["msc"], writes=allS)
            if not last:
                for hg in range(4):
                    S.act(lambda e, hg=hg: e.copy(out=sl4(Sgbf, hg), in_=sl4(Sg32, hg)), reads=["Sg32_%d" % hg], writes=["Sgbf%d" % hg])
            yield

        def finish_chunk(c, par, dr):
            cs = slice(c * 128, (c + 1) * 128)
            on = "oacc%d" % par
            if dr == 1:
                S.dma("pool", ob_scr[cs, :], fl(oacc[par]), reads=[on], writes=["ob_scr%d" % c])
                return
            S.dma("sp", fl(obl), ob_scr[cs, :], reads=["ob_scr%d" % c], writes=["obl"])
            S.dma("sp", fl(gzt), gz_tok[cs, :], writes=["gzt"])
            S.dve(lambda e: e.tensor_tensor(out=oacc[par], in0=oacc[par], in1=obl, op=ALU.add), reads=[on, "obl"], writes=[on])
            S.pool(lambda e: e.tensor_tensor(out=obl, in0=oacc[par], in1=oacc[par], op=ALU.mult), reads=[on], writes=["obl"])
            S.dve(lambda e: e.tensor_reduce(out=gsm[:, 0:16], in_=obl, axis=AX.X, op=ALU.add), reads=["obl"], writes=["gsm"])
            S.act(lambda e: e.activation(out=gsm[:, 16:32], in_=gsm[:, 0:16], func=AF.Sqrt, scale=1.0 / 128, bias=EPS), reads=["gsm"], writes=["gsm"])
            S.dve(lambda e: e.reciprocal(out=gsm[:, 32:48], in_=gsm[:, 16:32]), reads=["gsm"], writes=["gsm"])
            S.dve(lambda e: e.tensor_tensor(out=oacc[par], in0=oacc[par], in1=gsm[:, 32:48].unsqueeze(2).to_broadcast([128, 16, 128]), op=ALU.mult),
                  reads=[on, "gsm"], writes=[on])
            S.pool(lambda e: e.tensor_tensor(out=gzt, in0=gzt, in1=gnwb.unsqueeze(1).to_broadcast([128, 16, 128]), op=ALU.mult),
                   reads=["gzt", "gnwb"], writes=["gzt"])
            S.dve(lambda e: e.tensor_tensor(out=oacc[par], in0=oacc[par], in1=gzt, op=ALU.mult), reads=[on, "gzt"], writes=[on])
            S.dma("pool", mixed[cs, 2048:4096], fl(oacc[par]), reads=[on], writes=["mixed"])

        def drain(g):
            for _ in g:
                pass

        for dr in (1, 0):
            order = list(range(15, -1, -1)) if dr == 1 else list(range(16))
            init = s_gb if dr == 1 else s_gf
            S.dma("sp", Sg32, init.rearrange("h p v -> p h v"), writes=["Sg32_%d" % hg for hg in range(4)])
            for hg in range(4):
                S.act(lambda e, hg=hg: e.copy(out=sl4(Sgbf, hg), in_=sl4(Sg32, hg)), reads=["Sg32_%d" % hg], writes=["Sgbf%d" % hg])
            load_chunk(order[0], 0)
            drain(stageA(order[0], 0, dr))
            for idx, c in enumerate(order):
                par = idx % 2
                gB = stageB(c, par, dr)
                if idx + 1 < 16:
                    load_chunk(order[idx + 1], 1 - par)
                    gA = stageA(order[idx + 1], 1 - par, dr)
                else:
                    gA = iter(())
                doneA = doneB = False
                while not (doneA and doneB):
                    if not doneA:
                        try:
                            next(gA)
                        except StopIteration:
                            doneA = True
                    if not doneB:
                        try:
                            next(gB)
                        except StopIteration:
                            doneB = True
                finish_chunk(c, par, dr)
        S.barrier(bar[:, 0:1])
        if debug is not None and "stop3b" in debug:
            S.finish()
            S.emit(nc, st)
            return nc

        AR.top = persist_top
        gate_bc = AR.alloc([128, D])
        S.dma("sp", gate_bc, gate_d[0:1, :].to_broadcast([128, D]), writes=["gate_bc"])
        mixedT = AR.alloc([128, 32, 1024], BF16)
        mld = [AR.alloc([128, D // 2]) for _ in range(2)]
        wobf = [AR.alloc([128, 32, 512], BF16) for _ in range(2)]
        wstg = [AR.alloc([128, 4, 512]) for _ in range(2)]
        dstg = [AR.alloc([128, 4, 512]) for _ in range(2)]
        wq = [0]
        for tb in range(2):
            for tt in range(8):
                r0 = tb * 1024 + tt * 128
                for hf in range(2):
                    S.dma("sp", mld[hf], mixed[r0:r0 + 128, hf * 2048:(hf + 1) * 2048], writes=["mld%d" % hf])
                for g in range(8):
                    b = nb()
                    ml = mld[g // 4]
                    mln = "mld%d" % (g // 4)
                    for j in range(4):
                        kc = g * 4 + j
                        S.pe(lambda e, b=b, j=j, kc=kc, ml=ml: e.transpose(out=PS(b, j * 128, (j + 1) * 128), in_=ml[:, (kc % 16) * 128:(kc % 16 + 1) * 128],
                                                                            identity=ident), reads=[mln, "cst"], writes=[PB(b)])
                    dst = mixedT[:, g * 4:g * 4 + 4, tt * 128:(tt + 1) * 128]
                    src_ = lambda b=b: PS(b).rearrange("p (a t) -> p a t", a=4)
                    if g % 2 == 0:
                        S.act(lambda e, dst=dst, src_=src_: e.copy(out=dst, in_=src_()), writes=["mixedT", PB(b)])
                    else:
                        S.dve(lambda e, dst=dst, src_=src_: e.tensor_copy(out=dst, in_=src_()), writes=["mixedT", PB(b)])
            for cg in range(8):
                s = cg % 2
                for q in range(8):
                    ws = wstg[wq[0] % 2]
                    wsn = "wstg%d" % (wq[0] % 2)
                    wq[0] += 1
                    S.dma("sp", ws, w_out[q * 512:(q + 1) * 512, cg * 512:(cg + 1) * 512].rearrange("(kc p) c -> p kc c", p=128), writes=[wsn])
                    dstw = wobf[s][:, q * 4:(q + 1) * 4, :]
                    if q % 2 == 0:
                        S.act(lambda e, ws=ws, dstw=dstw: e.copy(out=dstw, in_=ws), reads=[wsn], writes=["wobf_%d_%d" % (s, q)])
                    else:
                        S.dve(lambda e, ws=ws, dstw=dstw: e.tensor_copy(out=dstw, in_=ws), reads=[wsn], writes=["wobf_%d_%d" % (s, q)])
                for tt in range(8):
                    dg = dstg[tt // 4]
                    dgn = "dstg%d" % (tt // 4)
                    b = nb()
                    for kc in range(32):
                        S.pe(lambda e, b=b, kc=kc, tt=tt, s=s: e.matmul(PS(b), lhsT=mixedT[:, kc, tt * 128:(tt + 1) * 128],
                                                                        rhs=wobf[s][:, kc, :], start=(kc == 0), stop=(kc == 31)),
                             reads=["mixedT", "wobf_%d_%d" % (s, kc // 4)], writes=[PB(b)])
                    S.dve(lambda e, b=b, tt=tt, dg=dg, cg=cg: e.tensor_tensor(out=dg[:, tt % 4, :], in0=PS(b),
                                                                             in1=gate_bc[:, cg * 512:(cg + 1) * 512], op=ALU.mult),
                          reads=["gate_bc"], writes=[dgn, PB(b)])
                    if tt % 4 == 3:
                        r1 = tb * 1024 + (tt // 4) * 512
                        S.dma("pool", delta[r1:r1 + 512, cg * 512:(cg + 1) * 512].rearrange("(tt p) c -> p tt c", p=128), dg,
                              reads=[dgn], writes=["delta"])
        S.barrier(bar[:, 0:1])
        AR.top = persist_top
        fnwb = AR.alloc([128, D])
        S.dma("sp", fnwb, fnw[0:1, :].to_broadcast([128, D]), writes=["fnwb"])
        xa = [AR.alloc([128, D]) for _ in range(3)]
        da = [AR.alloc([128, D]) for _ in range(3)]
        fsq = AR.alloc([128, 64])
        for tt in range(16):
            i = tt % 3
            rs = slice(tt * 128, (tt + 1) * 128)
            S.dma("sp", xa[i], x[rs, :], writes=["xa%d" % i])
            S.dma("sp", da[i], delta[rs, :], writes=["da%d" % i])
            S.dve(lambda e, i=i: e.tensor_tensor(out=xa[i], in0=xa[i], in1=da[i], op=ALU.add), reads=["xa%d" % i, "da%d" % i], writes=["xa%d" % i])
            S.act(lambda e, i=i, tt=tt: e.activation(out=da[i], in_=xa[i], func=AF.Square, accum_out=fsq[:, tt:tt + 1]),
                  reads=["xa%d" % i], writes=["da%d" % i, "fsq"])
            S.act(lambda e, tt=tt: e.activation(out=fsq[:, 16 + tt:17 + tt], in_=fsq[:, tt:tt + 1], func=AF.Sqrt, scale=1.0 / D, bias=EPS),
                  reads=["fsq"], writes=["fsq"])
            S.dve(lambda e, tt=tt: e.reciprocal(out=fsq[:, 32 + tt:33 + tt], in_=fsq[:, 16 + tt:17 + tt]), reads=["fsq"], writes=["fsq"])
            S.dve(lambda e, i=i, tt=tt: e.scalar_tensor_tensor(out=da[i], in0=xa[i], scalar=fsq[:, 32 + tt:33 + tt], in1=fnwb,
                                                               op0=ALU.mult, op1=ALU.mult),
                  reads=["xa%d" % i, "fsq", "fnwb"], writes=["da%d" % i])
            S.dma("pool", y[rs, :], da[i], reads=["da%d" % i], is_out=True)
        S.finish()
        S.emit(nc, st)
    return nc


def _consts():
    j = np.arange(128)[:, None].astype(np.float32)
    i = np.arange(128)[None, :].astype(np.float32)
    ident = (j == i).astype(np.float32)
    perm = np.zeros((128, 128), np.float32)
    for m in range(64):
        perm[m + 64, m] = -1.0
        perm[m, m + 64] = 1.0
    triF = (j <= i).astype(np.float32)
    triB = (j >= i).astype(np.float32)
    allowF = (i >= j).astype(np.float32)
    allowB = (i <= j).astype(np.float32)
    strictF = (i > j).astype(np.float32)
    strictB = (i < j).astype(np.float32)
    negmF = (allowF - 1.0) * 30000.0
    negmB = (allowB - 1.0) * 30000.0
    DFt = np.maximum(i - j, 0.0)
    DBt = np.maximum(j - i, 0.0)
    qeF = np.broadcast_to(i + 1.0, (128, 128)).astype(np.float32)
    qeB = np.broadcast_to(128.0 - i, (128, 128)).astype(np.float32)
    bd16 = ((j // 16) == (i // 16)).astype(np.float32)
    extra = [bd16]
    for b in (16, 32, 64):
        mx = (((j // (2 * b)) == (i // (2 * b))) & ((j // b) % 2 == 0) & ((i // b) % 2 == 1)).astype(np.float32)
        extra += [mx, np.ascontiguousarray(mx.T)]
    return np.ascontiguousarray(np.concatenate(
        [ident, perm, triF, triB, allowF, allowB, strictF, strictB, negmF, negmB, DFt, DBt, qeF, qeB] + extra, axis=1))


def _rope_tables(on):
    t = np.arange(NTOK)
    inv = (10000.0 ** (-np.arange(64, dtype=np.float32) / 64.0)).astype(np.float32)
    f = np.concatenate([inv, inv])[:, None]
    out = np.zeros((128, 192), np.float32)
    if on:
        row = np.arange(32, dtype=np.float32)[None, :]
        col = np.arange(64, dtype=np.float32)[None, :]
        out[:, 0:32] = np.cos(row * f)
        out[:, 32:64] = np.sin(row * f)
        out[:, 64:128] = np.cos(col * f)
        out[:, 128:192] = np.sin(col * f)
    else:
        out[:, 0:32] = 1.0
        out[:, 64:128] = 1.0
    return out


def _col_layout(v):
    return np.ascontiguousarray(v.reshape(32, 128).T)


def make_in_maps(inp):
    f32 = np.float32
    consts = _consts()
    p = np.arange(128, dtype=f32)
    shared = {
        "w_ada": np.ascontiguousarray(inp["w_ada"][0]),
        "b_ada": np.ascontiguousarray(inp["b_ada"][0][None, :]),
        "nw": _col_layout(inp["norm_w"][0]),
        "w_in": np.ascontiguousarray(inp["w_in"][0]),
        "w_out": np.ascontiguousarray(inp["w_out"][0]),
        "fnw": np.ascontiguousarray(inp["final_norm_w"][None, :]),
        "consts": consts,
        "retdec": np.concatenate([inp["ret_decay_fwd"][0], inp["ret_decay_bwd"][0]])[None, :].astype(f32),
        "convw": np.ascontiguousarray(inp["conv_w"][0].T.reshape(48, 128, 5).transpose(1, 0, 2)),
        "gpar": np.stack([np.concatenate([inp["gdn_a_log_fwd"][0], inp["gdn_a_log_bwd"][0]]),
                          np.concatenate([inp["gdn_dt_bias_fwd"][0], inp["gdn_dt_bias_bwd"][0]])], axis=1).astype(f32),
        "gnw": np.ascontiguousarray(inp["gdn_norm_w"][0][None, :]),
    }
    rope_on = _rope_tables(True)
    rope_off = _rope_tables(False)
    maps = []
    for c in range(8):
        m = dict(shared)
        misc = np.zeros((128, 8), f32)
        misc[:, 1] = 127.0 - p
        misc[:, 2] = p
        if c < 4:
            m["x"] = np.ascontiguousarray(inp["x_sample"][c])
            m["cv"] = _col_layout(inp["c"][c])
            misc[:, 0] = 1.0
            m["rope"] = rope_on
            m["s_rf"] = np.ascontiguousarray(inp["state_ret_fwd"][c, 0])
            m["s_rb"] = np.ascontiguousarray(inp["state_ret_bwd"][c, 0])
            m["s_gf"] = np.ascontiguousarray(inp["state_gdn_fwd"][c, 0])
            m["s_gb"] = np.ascontiguousarray(inp["state_gdn_bwd"][c, 0])
        else:
            j = c - 4
            m["x"] = np.ascontiguousarray(inp["x_prompt"][8 * j:8 * j + 8].reshape(NTOK, D))
            m["cv"] = _col_layout(inp["c_ctx"])
            m["rope"] = rope_off
            m["s_rf"] = np.zeros((8, 256, 256), f32)
            m["s_rb"] = np.zeros((8, 256, 256), f32)
            m["s_gf"] = np.zeros((16, 128, 128), f32)
            m["s_gb"] = np.zeros((16, 128, 128), f32)
        m["misc"] = misc
        maps.append(m)
    return maps


def kernel(**inputs):
    inp = {k: np.asarray(v) for k, v in inputs.items()}
    nc = build_nc()
    maps = make_in_maps(inp)
    res = run_bass_kernel_spmd(nc, maps, core_ids=list(range(8)))
    r = res.results
    y_sample = np.stack([r[c]["y"] for c in range(4)], axis=0)
    y_prompt = np.concatenate([r[c]["y"].reshape(8, 256, D) for c in range(4, 8)], axis=0)
    def gath(name, shp):
        return np.concatenate([r[c][name] for c in range(4, 8)], axis=0).reshape((32, 1) + shp)
    return (y_prompt.astype(np.float32), y_sample.astype(np.float32),
            gath("o_rf", (8, 256, 256)), gath("o_rb", (8, 256, 256)),
            gath("o_gf", (16, 128, 128)), gath("o_gb", (16, 128, 128)))
```
